# Optimizing a Trainium2 kernel written in Bass

```python
import jax, jax.numpy as jnp
from jax import lax
import numpy as np

D_MODEL = 1024
BATCH = 16
SEQ = 2048
DEPTH = 2

CHUNK = 64
Q_BLOCK = 128
N_MIXERS = 2
EPS = 1e-6
NEG_INF = -1e30

N_MEM = 256
MEM_HEADS = 4
MEM_HEAD_DIM = 64
MEM_WIDTH = MEM_HEADS * MEM_HEAD_DIM

POOL_WINDOWS = (2, 4, 8, 16)
POOL_GROUPS = len(POOL_WINDOWS)
POOL_GROUP_DIM = 192
POOL_WIDTH = POOL_GROUPS * POOL_GROUP_DIM

MLA_HEADS = 6
QK_NOPE = 128
QK_ROPE = 64
V_HEAD = 128
Q_LORA = 384
KV_LORA = 256
MLA_WIDTH = MLA_HEADS * V_HEAD
ROPE_THETA = 10000.0

MIX_WIDTH = POOL_WIDTH + MEM_WIDTH
POOL_IN = POOL_WIDTH + MEM_WIDTH + MIX_WIDTH
MLA_IN = Q_LORA + KV_LORA + QK_ROPE + MEM_WIDTH + MIX_WIDTH

N_POOL_LAYERS = (DEPTH + N_MIXERS - 1) // N_MIXERS
N_MLA_LAYERS = DEPTH // N_MIXERS

kernel_name = "hybrid_pool_mla_memory_trunk"


def rms_norm(x, g):
    x32 = x.astype(jnp.float32)
    y = x32 * lax.rsqrt(jnp.mean(x32 * x32, axis=-1, keepdims=True) + EPS)
    return (y * g.astype(jnp.float32)).astype(x.dtype)


def rope_tables(positions):
    inv_freq = ROPE_THETA ** (-(jnp.arange(0, QK_ROPE, 2, dtype=jnp.float32) / QK_ROPE))
    ang = positions.astype(jnp.float32)[..., None] * inv_freq
    return jnp.cos(ang), jnp.sin(ang)


def apply_rope(x, cos, sin):
    x32 = x.astype(jnp.float32)
    x1, x2 = jnp.split(x32, 2, axis=-1)
    out = jnp.concatenate([x1 * cos - x2 * sin, x2 * cos + x1 * sin], axis=-1)
    return out.astype(x.dtype)


def multi_scale_pool(u, w_group, scale):
    b, s, _ = u.shape
    u32 = u.astype(jnp.float32).reshape(b, s, POOL_GROUPS, POOL_GROUP_DIM)
    csum = jnp.cumsum(u32, axis=1)
    t = jnp.arange(s)
    pooled = []
    for gi, w in enumerate(POOL_WINDOWS):
        c = csum[:, :, gi]
        prev = jnp.pad(c, ((0, 0), (w, 0), (0, 0)))[:, :s]
        cnt = jnp.minimum(t + 1, w).astype(jnp.float32)
        pooled.append((c - prev) / cnt[None, :, None])
    mixed = jnp.stack(pooled, axis=2) - u32
    y = jnp.einsum('bsgc,gcd->bsgd', mixed.astype(u.dtype), w_group)
    return y.reshape(b, s, POOL_WIDTH) * scale


def memory_attention(q, mem_k, mem_v):
    b, s = q.shape[:2]
    sc = jnp.einsum('bqhd,bkhd->bhqk', q, mem_k).astype(jnp.float32) * (MEM_HEAD_DIM ** -0.5)
    p = jax.nn.softmax(sc, axis=-1).astype(mem_v.dtype)
    o = jnp.einsum('bhqk,bkhd->bqhd', p, mem_v)
    return o.reshape(b, s, MEM_WIDTH)


def mla_attention(q_nope, q_rope, k_nope, k_rope, v):
    s = q_nope.shape[1]
    scale = (QK_NOPE + QK_ROPE) ** -0.5
    outs = []
    for blk in range(s // Q_BLOCK):
        q0, q1 = blk * Q_BLOCK, (blk + 1) * Q_BLOCK
        kv_end = q1
        sc = (jnp.einsum('bqhd,bkhd->bhqk', q_nope[:, q0:q1], k_nope[:, :kv_end])
              + jnp.einsum('bqhd,bkd->bhqk', q_rope[:, q0:q1], k_rope[:, :kv_end]))
        q_chunk = (q0 + jnp.arange(Q_BLOCK)) // CHUNK
        k_chunk = jnp.arange(kv_end) // CHUNK
        mask = k_chunk[None, :] <= q_chunk[:, None]
        sc = jnp.where(mask, sc.astype(jnp.float32) * scale, NEG_INF)
        p = jax.nn.softmax(sc, axis=-1).astype(v.dtype)
        outs.append(jnp.einsum('bhqk,bkhd->bqhd', p, v[:, :kv_end]))
    return jnp.concatenate(outs, axis=1)


def setup_inputs(seed: int = 0) -> dict:
    key = jax.random.key(seed)
    ks = jax.random.split(key, 20)
    f32 = jnp.float32

    def nrm(k, shape, fan_in):
        return jax.random.normal(k, shape, f32) * (fan_in ** -0.5)

    def gain(k, shape):
        return 1.0 + 0.02 * jax.random.normal(k, shape, f32)

    x = jax.random.normal(ks[0], (BATCH, SEQ, D_MODEL), f32)
    mem = jax.random.normal(ks[1], (BATCH, N_MEM, D_MODEL), f32)
    start = jax.random.randint(ks[2], (BATCH, 1), 0, 4096, dtype=jnp.int32)
    positions = (start + jnp.arange(SEQ, dtype=jnp.int32)[None, :]).astype(jnp.int32)
    return {
        "x": x,
        "mem": mem,
        "positions": positions,
        "norm_g": gain(ks[3], (DEPTH, D_MODEL)),
        "mem_norm_g": gain(ks[4], (D_MODEL,)),
        "w_mem_kv": nrm(ks[5], (DEPTH, D_MODEL, 2 * MEM_WIDTH), D_MODEL),
        "w_out": nrm(ks[6], (DEPTH, MIX_WIDTH, D_MODEL), MIX_WIDTH),
        "pool_w_in": nrm(ks[7], (N_POOL_LAYERS, D_MODEL, POOL_IN), D_MODEL),
        "pool_w_group": nrm(ks[8], (N_POOL_LAYERS, POOL_GROUPS, POOL_GROUP_DIM, POOL_GROUP_DIM), POOL_GROUP_DIM),
        "pool_scale": gain(ks[9], (N_POOL_LAYERS, POOL_WIDTH)),
        "mla_w_in": nrm(ks[10], (N_MLA_LAYERS, D_MODEL, MLA_IN), D_MODEL),
        "mla_q_norm_g": gain(ks[11], (N_MLA_LAYERS, Q_LORA)),
        "mla_w_uq": nrm(ks[12], (N_MLA_LAYERS, Q_LORA, MLA_HEADS * (QK_NOPE + QK_ROPE)), Q_LORA),
        "mla_kv_norm_g": gain(ks[13], (N_MLA_LAYERS, KV_LORA)),
        "mla_w_ukv": nrm(ks[14], (N_MLA_LAYERS, KV_LORA, MLA_HEADS * (QK_NOPE + V_HEAD)), KV_LORA),
        "final_norm_g": gain(ks[15], (D_MODEL,)),
    }


def reference(x, mem, positions, norm_g, mem_norm_g, w_mem_kv, w_out, pool_w_in, pool_w_group,
              pool_scale, mla_w_in, mla_q_norm_g, mla_w_uq, mla_kv_norm_g, mla_w_ukv, final_norm_g):
    b, s, _ = x.shape
    cos, sin = rope_tables(positions)
    mem_n = rms_norm(mem, mem_norm_g)
    h = x
    for i in range(DEPTH):
        j = i // N_MIXERS
        hn = rms_norm(h, norm_g[i])
        mem_kv = jnp.einsum('bnd,de->bne', mem_n, w_mem_kv[i])
        mem_k, mem_v = jnp.split(mem_kv.reshape(b, N_MEM, 2, MEM_HEADS, MEM_HEAD_DIM), 2, axis=2)
        mem_k, mem_v = mem_k[:, :, 0], mem_v[:, :, 0]
        if i % N_MIXERS == 0:
            proj = jnp.einsum('bsd,de->bse', hn, pool_w_in[j])
            u, mq, gate = jnp.split(proj, [POOL_WIDTH, POOL_WIDTH + MEM_WIDTH], axis=-1)
            tok = multi_scale_pool(u, pool_w_group[j], pool_scale[j])
        else:
            proj = jnp.einsum('bsd,de->bse', hn, mla_w_in[j])
            c_q, c_kv, k_rope, mq, gate = jnp.split(
                proj, [Q_LORA, Q_LORA + KV_LORA, Q_LORA + KV_LORA + QK_ROPE,
                       Q_LORA + KV_LORA + QK_ROPE + MEM_WIDTH], axis=-1)
            q = jnp.einsum('bsr,re->bse', rms_norm(c_q, mla_q_norm_g[j]), mla_w_uq[j])
            q = q.reshape(b, s, MLA_HEADS, QK_NOPE + QK_ROPE)
            q_nope, q_rope = q[..., :QK_NOPE], q[..., QK_NOPE:]
            q_rope = apply_rope(q_rope, cos[:, :, None, :], sin[:, :, None, :])
            kv = jnp.einsum('bsr,re->bse', rms_norm(c_kv, mla_kv_norm_g[j]), mla_w_ukv[j])
            kv = kv.reshape(b, s, MLA_HEADS, QK_NOPE + V_HEAD)
            k_nope, v = kv[..., :QK_NOPE], kv[..., QK_NOPE:]
            k_rope = apply_rope(k_rope, cos, sin)
            tok = mla_attention(q_nope, q_rope, k_nope, k_rope, v).reshape(b, s, MLA_WIDTH)
        mem_o = memory_attention(mq.reshape(b, s, MEM_HEADS, MEM_HEAD_DIM), mem_k, mem_v)
        branch = jnp.concatenate([tok.astype(h.dtype), mem_o], axis=-1) * jax.nn.silu(gate)
        h = h + jnp.einsum('bse,ed->bsd', branch, w_out[i])
    return rms_norm(h, final_norm_g)
```

```python
import contextlib
import numpy as np
import concourse.bass as bass
import concourse.mybir as mybir
from concourse.bass_utils import run_bass_kernel_spmd

F32 = mybir.dt.float32
BF16 = mybir.dt.bfloat16
I32 = mybir.dt.int32
ACT = mybir.ActivationFunctionType
ALU = mybir.AluOpType

COMPUTE = ("pe", "act", "dve", "pool")
NDMASEM = 8
D = 1024
NMEM = 256
EPS = 1e-6
PI = float(np.pi)


class Op:
    __slots__ = ("eng", "fn", "reads", "writes", "dma", "deps", "sig", "sem", "val", "idx")

    def __init__(self, eng, fn, reads, writes, dma):
        self.eng, self.fn, self.reads, self.writes, self.dma = eng, fn, tuple(reads), tuple(writes), dma
        self.deps = []
        self.sig = False
        self.sem = None
        self.val = None


class Prog:
    def __init__(self):
        self.ops = []
        self.barriers = []

    def op(self, eng, fn, reads=(), writes=()):
        o = Op(eng, fn, reads, writes, False)
        self.ops.append(o)
        return o

    def dma(self, queue, out, in_, reads=(), writes=()):
        o = Op(queue, lambda e: e.dma_start(out=out, in_=in_), reads, writes, True)
        self.ops.append(o)
        return o

    def barrier(self):
        self.barriers.append(len(self.ops))

    def analyze(self):
        last_w = {}
        readers = {}
        bars = set(self.barriers)
        last_on = {}
        dmas_since = []
        pending_bar = {}
        for i, o in enumerate(self.ops):
            o.idx = i
            if i in bars:
                bd = list(last_on.values()) + list(dmas_since)
                dmas_since = []
                for e in COMPUTE + ("sp",):
                    pending_bar[e] = pending_bar.get(e, []) + bd
            deps = {}
            for k in o.reads:
                w = last_w.get(k)
                if w is not None:
                    deps[id(w)] = w
            for k in o.writes:
                w = last_w.get(k)
                if w is not None:
                    deps[id(w)] = w
                for r in readers.get(k, ()):
                    deps[id(r)] = r
            if o.eng in pending_bar:
                for d in pending_bar.pop(o.eng):
                    deps[id(d)] = d
            deps.pop(id(o), None)
            dl = []
            for d in deps.values():
                if (not d.dma) and (not o.dma) and d.eng == "pe" and o.eng == "pe":
                    continue
                dl.append(d)
            o.deps = dl
            for d in dl:
                d.sig = True
            for k in o.reads:
                lst = readers.setdefault(k, [])
                if not o.dma:
                    lst[:] = [r for r in lst if r.dma or r.eng != o.eng]
                lst.append(o)
            for k in o.writes:
                last_w[k] = o
                readers[k] = []
            if o.dma:
                dmas_since.append(o)
            else:
                last_on[o.eng] = o
        cnt = {e: 0 for e in COMPUTE}
        dcnt = {}
        self.final = {}
        for o in self.ops:
            if o.dma:
                j = dcnt.get(o.eng, 0)
                dcnt[o.eng] = j + 1
                o.sem = ("dma", o.eng, j % NDMASEM)
                o.val = 16 * (j // NDMASEM + 1)
                self.final[o.sem] = o.val
            elif o.sig:
                cnt[o.eng] += 1
                o.sem = ("eng", o.eng)
                o.val = cnt[o.eng]

    def sem_keys(self):
        ks = [("eng", e) for e in COMPUTE]
        for q in sorted({o.eng for o in self.ops if o.dma}):
            for j in range(NDMASEM):
                ks.append(("dma", q, j))
        return ks

    def emit(self, block, sems):
        per = {}
        for o in self.ops:
            per.setdefault(o.eng, []).append(o)
        final = self.final

        def run(engname, eng):
            seen = {}
            for o in per.get(engname, ()):
                need = {}
                for d in o.deps:
                    if need.get(d.sem, 0) < d.val:
                        need[d.sem] = d.val
                if o.dma and o.val > 16 and need.get(o.sem, 0) < o.val - 16:
                    need[o.sem] = o.val - 16
                for s, v in need.items():
                    if seen.get(s, 0) < v:
                        eng.wait_ge(sems[s], v)
                        seen[s] = v
                ins = o.fn(eng)
                if o.dma:
                    ins.then_inc(sems[o.sem], 16)
                elif o.sig:
                    ins.then_inc(sems[o.sem], 1)
            if engname == "sp":
                for s, v in final.items():
                    eng.wait_ge(sems[s], v)

        @block.tensor
        def _(e):
            run("pe", e)

        @block.scalar
        def _(e):
            run("act", e)

        @block.vector
        def _(e):
            run("dve", e)

        @block.gpsimd
        def _(e):
            run("pool", e)

        @block.sync
        def _(e):
            run("sp", e)


W1COLS = 384 + 256 + 128 + 128 + 256 + 1024
WQCOLS = 768 + 384 + 384
NGAIN = 8 + 8 + 8 + 8 + 6 + 3 + 2
NCST = 8 + 64


def build_nc(S=2048, NB=2, TT=256, phases=(0, 1)):
    NT = S // TT
    CPT = TT // 128
    nc = bass.Bass("TRN2", target_bir_lowering=False)
    P = Prog()

    def din(name, shape, dt=F32):
        return nc.dram_tensor(name, list(shape), dt, kind="ExternalInput").ap()

    xT = din("xT", [NB, D, S])
    memT = din("memT", [NB, D, NMEM])
    pos = din("pos", [NB, S], I32)
    gains_d = din("gains", [128, NGAIN])
    cst_d = din("cst", [128, NCST])
    ident_d = din("ident", [128, 128])
    mneg_d = din("mneg", [128, 2, 256])
    win0_d = din("win0", [128, 8, 2048])
    wout0_d = din("wout0", [128, 8, 1024])
    wbd_d = din("wbd", [128, 6, 768])
    wmkv0_d = din("wmkv0", [128, 8, 512])
    win1_d = din("win1", [128, 8, W1COLS])
    wout1_d = din("wout1", [128, 8, 1024])
    wq_d = din("wq", [128, 3, WQCOLS])
    wkv_d = din("wkv", [128, 2, 1536])
    wmkv1_d = din("wmkv1", [128, 8, 512])
    outT = nc.dram_tensor("outT", [NB, D, S], F32, kind="ExternalOutput").ap()
    if 0 in phases and 1 in phases:
        h1 = nc.dram_tensor("h1", [NB, D, S], F32).ap()
    elif 0 in phases:
        h1 = nc.dram_tensor("h1", [NB, D, S], F32, kind="ExternalOutput").ap()
    else:
        h1 = din("h1", [NB, D, S])

    ARENA0, ARENA1 = 16384 + 512, 229376 - 256
    cur = [ARENA0]
    hi = [ARENA0]

    uniq = [0]

    def sb(name, shape, dt):
        uniq[0] += 1
        name = "%s_%d" % (name, uniq[0])
        esz = 2 if dt == BF16 else 4
        n = int(np.prod(shape[1:])) * esz
        n = (n + 63) // 64 * 64
        t = nc.alloc_sbuf_tensor_at(name, list(shape), dt, offset=cur[0])
        cur[0] += n
        hi[0] = max(hi[0], cur[0])
        assert cur[0] <= ARENA1, ("SBUF overflow", name, cur[0])
        return t

    ps = [nc.alloc_psum_tensor("ps%d" % i, [128, 512], F32) for i in range(8)]
    rot = {"proj": [0, (0, 1)], "sc": [0, (2, 3)], "ao": [0, (4, 5)], "ad": [0, (6, 7)]}

    def bank(kind):
        r = rot[kind]
        b = r[1][r[0] % len(r[1])]
        r[0] += 1
        return b

    def mm(bk, n, lhsT, rhs, start, stop, reads):
        P.op("pe", lambda e: e.matmul(ps[bk][:, 0:n], lhsT, rhs, start=start, stop=stop), reads=reads, writes=[("ps", bk)])

    def mmo(out, bk, lhsT, rhs, start, stop, reads):
        P.op("pe", lambda e: e.matmul(out, lhsT, rhs, start=start, stop=stop), reads=reads, writes=[("ps", bk)])

    def act(out, in_, func, reads, writes, **kw):
        P.op("act", lambda e: e.activation(out, in_, func, **kw), reads=reads, writes=writes)

    def copy(eng, out, in_, reads, writes):
        if eng == "act":
            P.op("act", lambda e: e.activation(out, in_, ACT.Copy), reads=reads, writes=writes)
        else:
            P.op(eng, lambda e: e.tensor_copy(out, in_), reads=reads, writes=writes)

    def tt(eng, out, a, b, op, reads, writes):
        P.op(eng, lambda e: e.tensor_tensor(out, a, b, op), reads=reads, writes=writes)

    def ts(eng, out, a, s1, s2, op0, op1, reads, writes):
        if op1 is None:
            P.op(eng, lambda e: e.tensor_scalar(out, a, s1, None, op0), reads=reads, writes=writes)
        else:
            P.op(eng, lambda e: e.tensor_scalar(out, a, s1, s2, op0, op1), reads=reads, writes=writes)

    def stt(eng, out, a, s, b, op0, op1, reads, writes):
        P.op(eng, lambda e: e.scalar_tensor_tensor(out, a, s, b, op0, op1), reads=reads, writes=writes)

    def memset(eng, ap, v, writes):
        P.op(eng, lambda e: e.memset(ap, v), writes=writes)

    gains = sb("gains", [128, NGAIN], F32)
    cst = sb("cst", [128, NCST], F32)
    ones = sb("ones", [128, 128], BF16)
    ind = [sb("ind%d" % r, [128, 128], BF16) for r in range(2)]
    G0, G1, GF, GM, PSC, GQ, GKV = 0, 8, 16, 24, 32, 38, 41
    ident = sb("ident", [128, 128], BF16)
    mneg = sb("mneg", [128, 2, 256], BF16)
    P.dma("pool", ident[:, :], ident_d, writes=["ident"])
    P.dma("pool", mneg[:, :, :], mneg_d, writes=["mneg"])
    P.dma("sp", gains[:, :], gains_d, writes=["gains"])
    P.dma("sp", cst[:, :], cst_d, writes=["cst"])
    memset("pool", ones[:, :], 1.0, ["ones"])
    for r in range(2):
        memset("pool", ind[r][:, :], 0.0, [("ind", r)])
        memset("pool", ind[r][:, r * 64:(r + 1) * 64], 1.0, [("ind", r)])

    if 1 in phases:
        win1 = sb("win1", [128, 8, W1COLS], BF16)
        wq = sb("wq", [128, 3, WQCOLS], BF16)
        wkv = sb("wkv", [128, 2, 1536], BF16)
        wmkv1 = sb("wmkv1", [128, 8, 512], BF16)
    shared0 = cur[0]

    def load_w(t, d, nch, key):
        for c in range(nch):
            P.dma("pool", t[:, c, :], d[:, c, :], writes=[(key, c)])

    def load_l1_weights():
        load_w(wmkv1, wmkv1_d, 8, "wmkv1")
        load_w(win1, win1_d, 8, "win1")
        load_w(wq, wq_d, 3, "wq")
        load_w(wkv, wkv_d, 2, "wkv")

    def rms_stats(B, src, srckeys, nch, n, ncols, tag):
        bk = bank("proj")
        for c in range(nch):
            sq = B["sq"][c % 2]
            act(sq[:, 0:ncols], src(c), ACT.Square, reads=[srckeys(c)], writes=[("sq", c % 2)])
            mm(bk, ncols, ones[:, :], sq[:, 0:ncols], c == 0, c == nch - 1, reads=["ones", ("sq", c % 2)])
        rstd = B["rstd" + tag]
        rk = "rstd" + tag
        act(rstd[:, 0:ncols], ps[bk][:, 0:ncols], ACT.Ln, reads=[("ps", bk)], writes=[rk], bias=EPS, scale=1.0 / n)
        act(rstd[:, 0:ncols], rstd[:, 0:ncols], ACT.Exp, reads=[rk], writes=[rk], scale=-0.5)

    def mem_prep(B, b, wmkv, wkey):
        memKm, memVp = B["memKm"][b], B["memVp"][b]
        memn = B["br"]
        mem32 = B["X"][1]
        assert TT == NMEM
        P.dma("sp", mem32[:, :, :], memT[b].rearrange("(c p) t -> p c t", p=128), writes=[("X1", c) for c in range(8)])
        rms_stats(B, lambda c: mem32[:, c, :], lambda c: ("X1", c), 8, D, NMEM, "h")
        for c in range(8):
            stt("dve", memn[:, c, :], mem32[:, c, :], gains[:, GM + c:GM + c + 1], B["rstdh"][:, 0:NMEM], ALU.mult, ALU.mult,
                reads=[("X1", c), "gains", "rstdh"], writes=[("br", c)])
        for j in range(2):
            bk = bank("proj")
            for c in range(8):
                mm(bk, NMEM, wmkv[:, c, j * 128:(j + 1) * 128], memn[:, c, :], c == 0, c == 7, reads=[(wkey, c), ("br", c)])
            copy("act", memKm[0][0:64, j, :], ps[bk][0:64, 0:NMEM], reads=[("ps", bk)], writes=[("memKm", b, 0)])
            copy("dve", memKm[1][64:128, j, :], ps[bk][64:128, 0:NMEM], reads=[("ps", bk)], writes=[("memKm", b, 1)])
        for kc in range(2):
            bk = bank("proj")
            for c in range(8):
                mm(bk, 256, memn[:, c, kc * 128:(kc + 1) * 128], wmkv[:, c, 256:512], c == 0, c == 7, reads=[(wkey, c), ("br", c)])
            src = ps[bk][:, 0:256].rearrange("p (j r d) -> p j r d", j=2, r=2)
            for r in range(2):
                copy("act" if r == 0 else "dve", memVp[:, kc, :, r, r * 64:(r + 1) * 64], src[:, :, r, :],
                     reads=[("ps", bk)], writes=[("memVp", b)])

    def finish_head(B, bo, bd, c, par):
        rden, rs = B["rden"], B["rs"]
        k = B["rr"][0] % 2
        B["rr"][0] += 1
        act(rden[k][:, :], ps[bd][:, 0:TT], ACT.Ln, reads=[("ps", bd)], writes=[("rden", k)])
        act(rden[k][:, :], rden[k][:, :], ACT.Exp, reads=[("rden", k)], writes=[("rden", k)], scale=-1.0)
        tt("dve", rs[k][:, :], rden[k][:, :], B["sg"][par][:, c, :], ALU.mult, reads=[("rden", k), ("sg", par, c)], writes=[("rs", k)])
        tt("dve", B["br"][:, c, :], ps[bo][:, 0:TT], rs[k][:, :], ALU.mult, reads=[("ps", bo), ("rs", k)], writes=[("br", c)])

    def next_e(B):
        k = B["er"][0] % 3
        B["er"][0] += 1
        return k

    def mem_attn_units(B, b, par):
        units = []
        mq, memKm, memVp = B["mq"][par], B["memKm"][b], B["memVp"][b]
        for j in range(2):
            def u(j=j):
                bo, bd = bank("ao"), bank("ad")

                def score(r):
                    sbk = bank("sc")
                    for kc in range(2):
                        mmo(ps[sbk][:, kc * TT:(kc + 1) * TT], sbk, memKm[r][:, j, kc * 128:(kc + 1) * 128], mq[:, j, :], True, True,
                            reads=[("memKm", b, r), ("mq", par, j)])
                    return sbk

                pend = score(0)
                n = 0
                for r in range(2):
                    sbk = pend
                    if r == 0:
                        pend = score(1)
                    ek = next_e(B)
                    E = B["E"][ek]
                    act(E[:, :], ps[sbk][:, 0:2 * TT], ACT.Exp, reads=[("ps", sbk)], writes=[("E", ek)], scale=0.125)
                    for kc in range(2):
                        mm(bo, TT, memVp[:, kc, j, r, :], E[:, kc * TT:(kc + 1) * TT], n == 0, n == 3, reads=[("memVp", b), ("E", ek)])
                        mm(bd, TT, ind[r][:, :], E[:, kc * TT:(kc + 1) * TT], n == 0, n == 3, reads=[("ind", r), ("E", ek)])
                        n += 1
                finish_head(B, bo, bd, 6 + j, par)
            units.append((12, u))
        return units

    def out_proj_units(B, X, xk, wout, wkey):
        units = []
        for co in range(8):
            def u(co=co):
                bk = bank("proj")
                for ci in range(8):
                    mm(bk, TT, wout[:, ci, co * 128:(co + 1) * 128], B["br"][:, ci, :], ci == 0, ci == 7, reads=[(wkey, ci), ("br", ci)])
                tt("dve", X[:, co, :], X[:, co, :], ps[bk][:, 0:TT], ALU.add, reads=[(xk, co), ("ps", bk)], writes=[(xk, co)])
            units.append((8, u))
        return units

    def norm_in(B, X, xk, goff):
        rms_stats(B, lambda c: X[:, c, :], lambda c: (xk, c), 8, D, TT, "h")
        for c in range(8):
            stt("dve", B["hn"][:, c, :], X[:, c, :], gains[:, goff + c:goff + c + 1], B["rstdh"][:, :], ALU.mult, ALU.mult,
                reads=[(xk, c), "gains", "rstdh"], writes=[("hn", c)])

    def common_bufs(ph):
        B = {}
        B["sq"] = [sb("sq%d" % k, [128, TT], BF16) for k in range(2)]
        for tg in (("h",) if ph == 0 else ("h", "q", "kv")):
            B["rstd" + tg] = sb("rstd" + tg, [128, TT], F32)
        B["hn"] = sb("hn", [128, 8, TT], BF16)
        B["mq"] = [sb("mq%d" % k, [128, 2, TT], BF16) for k in range(2)]
        B["sg"] = [sb("sg%d" % k, [128, 8, TT], BF16) for k in range(2)]
        B["br"] = sb("br", [128, 8, TT], BF16)
        B["E"] = [sb("E%d" % k, [128, 2 * TT], BF16) for k in range(3)]
        B["er"] = [0]
        B["rr"] = [0]
        B["rden"] = [sb("rden%d" % k, [128, TT], F32) for k in range(2)]
        B["rs"] = [sb("rs%d" % k, [128, TT], F32) for k in range(2)]
        B["memKm"] = [[sb("memKm%d_%d" % (b, r), [128, 2, NMEM], BF16) for r in range(2)] for b in range(NB)]
        B["memVp"] = [sb("memVp%d" % b, [128, 2, 2, 2, 128], BF16) for b in range(NB)]
        B["X"] = [sb("X%d" % k, [128, 8, TT], F32) for k in range(3 if ph == 0 else 2)]
        for b in range(NB):
            for r in range(2):
                memset("pool", B["memKm"][b][r][:, :, :], 0.0, [("memKm", b, r)])
            memset("pool", B["memVp"][b][:, :, :, :, :], 0.0, [("memVp", b)])
        return B

    def run_units(units):
        for _, f in units:
            f()

    def merge_units(ua, ub, A_OFFSET):
        items = []
        for lst, tag in ((ua, 0), (ub, 1)):
            tot = float(sum(w for w, _ in lst)) or 1.0
            acc = 0.0
            off = A_OFFSET if tag == 0 else 0.0
            for w, f in lst:
                items.append((off + (1.0 - off) * (acc + 0.5 * w) / tot, tag, f))
                acc += w
        items.sort(key=lambda t: (t[0], t[1]))
        for _, _, f in items:
            f()

    def pipeline(stageA, stageB, ntiles, overlap_ok=lambda n: True, a_off=0.15):
        import os
        if os.environ.get("NOPIPE"):
            overlap_ok = lambda n: False
        run_units(stageA(0))
        for n in range(ntiles):
            ub = stageB(n)
            if n + 1 < ntiles and overlap_ok(n + 1):
                merge_units(stageA(n + 1), ub, a_off)
            else:
                run_units(ub)
                if n + 1 < ntiles:
                    run_units(stageA(n + 1))

    NTILES = NB * NT
    A_OFFSET = 0.15

    if 0 in phases:
        cur[0] = shared0
        win0 = sb("win0", [128, 8, 2048], BF16)
        wout0 = sb("wout0", [128, 8, 1024], BF16)
        wbd = sb("wbd", [128, 6, 768], BF16)
        wmkv0 = sb("wmkv0", [128, 8, 512], BF16)
        load_w(wmkv0, wmkv0_d, 8, "wmkv0")
        load_w(win0, win0_d, 8, "win0")
        load_w(wbd, wbd_d, 6, "wbd")
        load_w(wout0, wout0_d, 8, "wout0")
        B = common_bufs(0)
        UW = 16 + TT
        ub_ = [sb("u%d" % c, [128, UW], F32) for c in range(6)]
        tp = [[sb("tp%d_%d" % (k, l), [128, UW], F32) for l in range(2)] for k in range(3)]
        mixed = [sb("mixed%d" % k, [128, 6, TT], BF16) for k in range(2)]
        fix = sb("fix", [128, 16], F32)
        SEG = {0: [(0, 128, 1)], 1: [(0, 64, 1), (64, 128, 2)], 2: [(0, 128, 2)],
               3: [(0, 128, 3)], 4: [(0, 64, 3), (64, 128, 4)], 5: [(0, 128, 4)]}
        GL = []
        for co in range(6):
            gs = {(128 * co) // 192, (128 * co + 127) // 192}
            cis = sorted({ci for g in gs for ci in range((192 * g) // 128, (192 * g + 191) // 128 + 1)})
            GL.append(cis)

        def load_x(n):
            if n >= NTILES:
                return
            b_, i_ = divmod(n, NT)
            P.dma("sp", B["X"][n % 3][:, :, :], xT[b_].rearrange("(c p) t -> p c t", p=128)[:, :, i_ * TT:(i_ + 1) * TT],
                  writes=[("X%d" % (n % 3), c) for c in range(8)])

        for b in range(NB):
            mem_prep(B, b, wmkv0, "wmkv0")
        load_x(0)
        load_x(1)

        def stageA0(n):
            b, i = divmod(n, NT)
            par = n % 2
            X, xk = B["X"][n % 3], "X%d" % (n % 3)
            units = []

            def u_pre():
                if n >= 1:
                    load_x(n + 1)
                if n == 1 and 1 in phases:
                    load_l1_weights()
                if i == 0:
                    for c in range(6):
                        memset("pool", ub_[c][:, 0:16], 0.0, [("u", c)])
            units.append((0, u_pre))
            pend_fin = {}
            for gi in range(16):
                def u(gi=gi):
                    bk = bank("proj")
                    for c in range(8):
                        mm(bk, TT, win0[:, c, gi * 128:(gi + 1) * 128], B["hn"][:, c, :], c == 0, c == 7, reads=[("win0", c), ("hn", c)])
                    if gi < 6:
                        copy("act", ub_[gi][:, 16:UW], ps[bk][:, 0:TT], reads=[("ps", bk)], writes=[("u", gi)])
                    elif gi < 8:
                        copy("dve", B["mq"][par][:, gi - 6, :], ps[bk][:, 0:TT], reads=[("ps", bk)], writes=[("mq", par, gi - 6)])
                    else:
                        copy("dve" if gi % 2 else "act", B["sg"][par][:, gi - 8, :], ps[bk][:, 0:TT], reads=[("ps", bk)], writes=[("sg", par, gi - 8)])
                    if 0 <= gi - 3 < 6:
                        pool_final(gi - 3)
                    if gi < 6:
                        pool_adds(gi)
                units.append((8, u))

            def u_silu():
                for j in range(8):
                    act(B["sg"][par][:, j, :], B["sg"][par][:, j, :], ACT.Silu, reads=[("sg", par, j)], writes=[("sg", par, j)])
            units.append((1, u_silu))

            def pool_adds(c):
                k = c % 3
                U = ub_[c]
                segs = []
                for (r0, r1, L) in SEG[c]:
                    srcb, srck = U, [("u", c)]
                    for l in range(1, L + 1):
                        dst = tp[k][(l - 1) % 2]
                        dk = [("tp", k, (l - 1) % 2, hh) for hh in (0, 64) if r0 <= hh < r1]
                        lo = (1 << l) - 1
                        sh = 1 << (l - 1)
                        tt("pool", dst[r0:r1, lo:UW], srcb[r0:r1, lo:UW], srcb[r0:r1, lo - sh:UW - sh], ALU.add,
                           reads=srck, writes=dk)
                        srcb, srck = dst, dk
                    segs.append((r0, r1, L, srcb, srck))
                pend_fin[c] = segs

            def pool_final(c):
                mx = mixed[par]
                U = ub_[c]
                for (r0, r1, L, srcb, srck) in pend_fin[c]:
                    w = 1 << L
                    stt("dve", mx[r0:r1, c, :], srcb[r0:r1, 16:UW], 1.0 / w, U[r0:r1, 16:UW], ALU.mult, ALU.subtract,
                        reads=srck + [("u", c)], writes=[("mixed", par, c)])
                    if i == 0:
                        tt("pool", fix[r0:r1, :], srcb[r0:r1, 16:32], cst[r0:r1, 8 + 16 * (L - 1):8 + 16 * L], ALU.mult,
                           reads=srck + ["cst"], writes=["fix"])
                        tt("pool", mx[r0:r1, c, 0:16], fix[r0:r1, :], U[r0:r1, 16:32], ALU.subtract,
                           reads=["fix", ("u", c)], writes=[("mixed", par, c)])
                copy("pool", U[:, 0:16], U[:, TT:UW], reads=[("u", c)], writes=[("u", c)])

            def u_norm_next():
                if n + 1 < NTILES:
                    norm_in(B, B["X"][(n + 1) % 3], "X%d" % ((n + 1) % 3), G0)
            units.append((8, u_norm_next))
            return units

        def stageB0(n):
            b, i = divmod(n, NT)
            par = n % 2
            X, xk = B["X"][n % 3], "X%d" % (n % 3)
            units = []
            for co in range(6):
                def u(co=co):
                    bk = bank("proj")
                    cis = GL[co]
                    for k_, ci in enumerate(cis):
                        mm(bk, TT, wbd[:, ci, co * 128:(co + 1) * 128], mixed[par][:, ci, :], k_ == 0, k_ == len(cis) - 1,
                           reads=[("wbd", ci), ("mixed", par, ci)])
                    stt("dve", B["br"][:, co, :], ps[bk][:, 0:TT], gains[:, PSC + co:PSC + co + 1], B["sg"][par][:, co, :], ALU.mult, ALU.mult,
                        reads=[("ps", bk), "gains", ("sg", par, co)], writes=[("br", co)])
                units.append((2, u))
            units += mem_attn_units(B, b, par)
            units += out_proj_units(B, X, xk, wout0, "wout0")

            def u_store():
                P.dma("sp", h1[b].rearrange("(c p) t -> p c t", p=128)[:, :, i * TT:(i + 1) * TT], X[:, :, :],
                      reads=[(xk, c) for c in range(8)], writes=[("h1", b, i)])
            units.append((0, u_store))
            return units

        nc._p0_end = cur[0]
        norm_in(B, B["X"][0], "X0", G0)
        pipeline(stageA0, stageB0, NTILES, a_off=0.0)
        P.barrier()

    if 1 in phases:
        cur[0] = shared0
        if 0 not in phases:
            load_l1_weights()
        wout1 = sb("wout1", [128, 8, 1024], BF16)
        load_w(wout1, wout1_d, 8, "wout1")
        B = common_bufs(1)
        KnT = sb("KnT", [128, 6, S], BF16)
        krT = [sb("krT%d" % r, [128, S], BF16) for r in range(2)]
        V = sb("V", [128, S // 128, 768], BF16)
        cq = sb("cq", [128, 3, TT], BF16)
        ckv = sb("ckv", [128, 2, TT], BF16)
        cqn = sb("cqn", [128, 3, TT], BF16)
        ckvn = sb("ckvn", [128, 2, TT], BF16)
        qn = [sb("qn%d" % k, [128, 6, TT], BF16) for k in range(2)]
        qr = [sb("qr%d" % k, [128, 3, TT], BF16) for k in range(2)]
        posf = sb("posf", [128, TT], F32)
        ang = sb("ang", [128, TT], F32)
        ang2 = sb("ang2", [128, TT], F32)
        cs = sb("cs", [128, TT], F32)
        sn = sb("sn", [128, TT], F32)
        rt = [sb("rt%d" % k, [128, TT], F32) for k in range(2)]
        posi = ang[:, :].bitcast(I32)
        ki = rt[1][:, :].bitcast(I32)
        kf = rt[0]
        for r in range(2):
            memset("pool", krT[r][:, :], 0.0, [("kr", r, i) for i in range(NT)])
        SCALE = float(192.0 ** -0.5)

        def trig_arg(a, akey, phase_col):
            ts("dve", a[:, :], posf[:, :], cst[:, 0:1], cst[:, phase_col:phase_col + 1], ALU.mult, ALU.add,
               reads=["posf", "cst"], writes=[akey])
            ts("dve", ki, a[:, :], 1.0 / (2 * PI), None, ALU.mult, None, reads=[akey], writes=[("rt", 1)])
            copy("dve", kf[:, :], ki, reads=[("rt", 1)], writes=[("rt", 0)])
            stt("dve", a[:, :], kf[:, :], -2 * PI, a[:, :], ALU.mult, ALU.add, reads=[("rt", 0), akey], writes=[akey])
            ts("dve", a[:, :], a[:, :], -PI, PI, ALU.max, ALU.min, reads=[akey], writes=[akey])

        def rope_mul(bk_raw, bk_swp):
            tt("dve", rt[0][:, :], ps[bk_raw][:, 0:TT], cs[:, :], ALU.mult, reads=[("ps", bk_raw), "cs"], writes=[("rt", 0)])
            tt("dve", rt[1][:, :], ps[bk_swp][:, 0:TT], sn[:, :], ALU.mult, reads=[("ps", bk_swp), "sn"], writes=[("rt", 1)])

        def load_h(n):
            if n >= NTILES:
                return
            b_, i_ = divmod(n, NT)
            P.dma("sp", B["X"][n % 2][:, :, :], h1[b_].rearrange("(c p) t -> p c t", p=128)[:, :, i_ * TT:(i_ + 1) * TT],
                  reads=[("h1", b_, i_)], writes=[("X%d" % (n % 2), c) for c in range(8)])

        for b in range(NB):
            mem_prep(B, b, wmkv1, "wmkv1")
        load_h(0)
        load_h(1)

        def proj8(col, hn):
            bk = bank("proj")
            for c in range(8):
                mm(bk, TT, win1[:, c, col:col + 128], hn[:, c, :], c == 0, c == 7, reads=[("win1", c), ("hn", c)])
            return bk

        def stageA1(n):
            b, i = divmod(n, NT)
            par = n % 2
            t0 = i * TT
            X, xk = B["X"][par], "X%d" % par
            hn = B["hn"]
            units = []

            def u_norm():
                P.dma("sp", posi, pos[b:b + 1, t0:t0 + TT].partition_broadcast(128), writes=["ang"])
                copy("dve", posf[:, :], posi, reads=["ang"], writes=["posf"])
                trig_arg(ang, "ang", 1)
                trig_arg(ang2, "ang2", 2)
                norm_in(B, X, xk, G1)
            units.append((24, u_norm))
            for j in range(8):
                def u(j=j):
                    bk = proj8(1152 + 128 * j, hn)
                    copy("dve", B["sg"][par][:, j, :], ps[bk][:, 0:TT], reads=[("ps", bk)], writes=[("sg", par, j)])
                units.append((8, u))

            def u_silu_block():
                for j in range(8):
                    act(B["sg"][par][:, j, :], B["sg"][par][:, j, :], ACT.Silu, reads=[("sg", par, j)], writes=[("sg", par, j)])
                act(cs[:, :], ang[:, :], ACT.Sin, reads=["ang"], writes=["cs"])
                act(sn[:, :], ang2[:, :], ACT.Sin, reads=["ang2"], writes=["sn"])
                for j in range(3):
                    bk = proj8(128 * j, hn)
                    copy("dve", cq[:, j, :], ps[bk][:, 0:TT], reads=[("ps", bk)], writes=[("cq", j)])
                for j in range(2):
                    bk = proj8(384 + 128 * j, hn)
                    copy("dve", ckv[:, j, :], ps[bk][:, 0:TT], reads=[("ps", bk)], writes=[("ckv", j)])
                for j in range(2):
                    bk = proj8(896 + 128 * j, hn)
                    copy("dve", B["mq"][par][:, j, :], ps[bk][:, 0:TT], reads=[("ps", bk)], writes=[("mq", par, j)])
            units.append((56, u_silu_block))

            def u_lat():
                rms_stats(B, lambda c: cq[:, c, :], lambda c: ("cq", c), 3, 384, TT, "q")
                for j in range(3):
                    stt("dve", cqn[:, j, :], cq[:, j, :], gains[:, GQ + j:GQ + j + 1], B["rstdq"][:, :], ALU.mult, ALU.mult,
                        reads=[("cq", j), "gains", "rstdq"], writes=[("cqn", j)])
                rms_stats(B, lambda c: ckv[:, c, :], lambda c: ("ckv", c), 2, 256, TT, "kv")
                for j in range(2):
                    stt("dve", ckvn[:, j, :], ckv[:, j, :], gains[:, GKV + j:GKV + j + 1], B["rstdkv"][:, :], ALU.mult, ALU.mult,
                        reads=[("ckv", j), "gains", "rstdkv"], writes=[("ckvn", j)])
            units.append((5, u_lat))

            def u_kr():
                bkr = proj8(640, hn)
                bks = proj8(768, hn)
                rope_mul(bkr, bks)
                tt("pool", krT[0][0:64, t0:t0 + TT], rt[0][0:64, :], rt[1][0:64, :], ALU.add,
                   reads=[("rt", 0), ("rt", 1)], writes=[("kr", 0, i)])
                tt("pool", krT[1][64:128, t0:t0 + TT], rt[0][64:128, :], rt[1][64:128, :], ALU.add,
                   reads=[("rt", 0), ("rt", 1)], writes=[("kr", 1, i)])
            units.append((16, u_kr))
            for h in range(6):
                def u(h=h):
                    bk = bank("proj")
                    for c in range(2):
                        mm(bk, TT, wkv[:, c, h * 128:(h + 1) * 128], ckvn[:, c, :], c == 0, c == 1, reads=[("wkv", c), ("ckvn", c)])
                    copy("act" if h % 2 == 0 else "dve", KnT[:, h, t0:t0 + TT], ps[bk][:, 0:TT], reads=[("ps", bk)], writes=[("kn", h, i)])
                units.append((2, u))
            for s_ in range(CPT):
                def u(s_=s_):
                    kc = i * CPT + s_
                    for (o0, ncol) in ((0, 512), (512, 256)):
                        bk = bank("proj")
                        for c in range(2):
                            mm(bk, ncol, ckvn[:, c, s_ * 128:(s_ + 1) * 128], wkv[:, c, 768 + o0:768 + o0 + ncol], c == 0, c == 1,
                               reads=[("wkv", c), ("ckvn", c)])
                        copy("act" if o0 == 0 else "dve", V[:, kc, o0:o0 + ncol], ps[bk][:, 0:ncol], reads=[("ps", bk)], writes=[("V", kc)])
                units.append((6, u))
            for h in range(6):
                def u(h=h):
                    bk = bank("proj")
                    for c in range(3):
                        mm(bk, TT, wq[:, c, h * 128:(h + 1) * 128], cqn[:, c, :], c == 0, c == 2, reads=[("wq", c), ("cqn", c)])
                    copy("act" if h % 2 == 0 else "dve", qn[par][:, h, :], ps[bk][:, 0:TT], reads=[("ps", bk)], writes=[("qn", par, h)])
                units.append((3, u))
            for j in range(3):
                def u(j=j):
                    bkr = bank("proj")
                    for c in range(3):
                        mm(bkr, TT, wq[:, c, 768 + j * 128:768 + (j + 1) * 128], cqn[:, c, :], c == 0, c == 2, reads=[("wq", c), ("cqn", c)])
                    bks = bank("proj")
                    for c in range(3):
                        mm(bks, TT, wq[:, c, 1152 + j * 128:1152 + (j + 1) * 128], cqn[:, c, :], c == 0, c == 2, reads=[("wq", c), ("cqn", c)])
                    rope_mul(bkr, bks)
                    tt("pool", qr[par][:, j, :], rt[0][:, :], rt[1][:, :], ALU.add, reads=[("rt", 0), ("rt", 1)], writes=[("qr", par, j)])
                units.append((6, u))
            return units

        def stageB1(n):
            b, i = divmod(n, NT)
            par = n % 2
            t0 = i * TT
            X, xk = B["X"][par], "X%d" % par
            units = []
            for h in range(6):
                def u(h=h):
                    bo, bd = bank("ao"), bank("ad")
                    chunks = [("d", c_) for c_ in range(CPT)] + [("f", kc) for kc in range(i * CPT)]
                    assert CPT == 2
                    pairs = [chunks[k:k + 2] for k in range(0, len(chunks), 2)]
                    npair = len(pairs)

                    def score(pidx):
                        sbk = bank("sc")
                        info = []
                        for half, (kind, v) in enumerate(pairs[pidx]):
                            kc = i * CPT + v if kind == "d" else v
                            ti = kc // CPT
                            out = ps[sbk][:, half * TT:(half + 1) * TT]
                            diag = kind == "d"
                            mmo(out, sbk, KnT[:, h, kc * 128:(kc + 1) * 128], qn[par][:, h, :], True, False,
                                reads=[("kn", h, ti), ("qn", par, h)])
                            mmo(out, sbk, krT[h % 2][:, kc * 128:(kc + 1) * 128], qr[par][:, h // 2, :], False, not diag,
                                reads=[("kr", h % 2, ti), ("qr", par, h // 2)])
                            if diag:
                                mmo(out, sbk, ident[:, :], mneg[:, v, :], False, True, reads=["ident", "mneg"])
                            info.append(kc)
                        return sbk, info

                    pend = score(0)
                    nmm = 0
                    for pidx in range(npair):
                        sbk, info = pend
                        if pidx + 1 < npair:
                            pend = score(pidx + 1)
                        ek = next_e(B)
                        E = B["E"][ek]
                        act(E[:, :], ps[sbk][:, 0:2 * TT], ACT.Exp, reads=[("ps", sbk)], writes=[("E", ek)], scale=SCALE)
                        for half, kc in enumerate(info):
                            first, last = nmm == 0, nmm == 2 * npair - 1
                            mm(bo, TT, V[:, kc, h * 128:(h + 1) * 128], E[:, half * TT:(half + 1) * TT], first, last, reads=[("V", kc), ("E", ek)])
                            mm(bd, TT, ones[:, :], E[:, half * TT:(half + 1) * TT], first, last, reads=["ones", ("E", ek)])
                            nmm += 1
                    finish_head(B, bo, bd, h, par)
                units.append((4 * (i * CPT + CPT), u))
            units += mem_attn_units(B, b, par)
            units += out_proj_units(B, X, xk, wout1, "wout1")

            def u_final():
                rms_stats(B, lambda c: X[:, c, :], lambda c: (xk, c), 8, D, TT, "h")
                for c in range(8):
                    stt("dve", X[:, c, :], X[:, c, :], gains[:, GF + c:GF + c + 1], B["rstdh"][:, :], ALU.mult, ALU.mult,
                        reads=[(xk, c), "gains", "rstdh"], writes=[(xk, c)])
                P.dma("sp", outT[b].rearrange("(c p) t -> p c t", p=128)[:, :, t0:t0 + TT], X[:, :, :],
                      reads=[(xk, c) for c in range(8)])
                load_h(n + 2)
            units.append((8, u_final))
            return units

        nc._p1_end = cur[0]
        pipeline(stageA1, stageB1, NTILES, overlap_ok=lambda n: n % NT != 0, a_off=0.05)

    P.analyze()
    with contextlib.ExitStack() as st:
        sems = {k: st.enter_context(nc.semaphore("s_" + "_".join(map(str, k)))) for k in P.sem_keys()}
        block = st.enter_context(nc.Block())
        P.emit(block, sems)
    nc._n_ops = len(P.ops)
    nc._sbuf_hi = hi[0]
    return nc


def _chunked(w):
    k, n = w.shape
    return np.ascontiguousarray(w.reshape(k // 128, 128, n).transpose(1, 0, 2))


def _cols(v):
    return v.reshape(-1, 128).T


def prep_shared(inp):
    f = np.float32
    sh = {}
    sh["win0"] = _chunked(inp["pool_w_in"][0])
    sh["wout0"] = _chunked(inp["w_out"][0])
    sh["wout1"] = _chunked(inp["w_out"][1])
    sh["wmkv0"] = _chunked(inp["w_mem_kv"][0])
    sh["wmkv1"] = _chunked(inp["w_mem_kv"][1])
    wbd = np.zeros((768, 768), f)
    for g in range(4):
        wbd[192 * g:192 * g + 192, 192 * g:192 * g + 192] = inp["pool_w_group"][0, g]
    sh["wbd"] = _chunked(wbd)
    w = inp["mla_w_in"][0]
    kr = w[:, 640:704]
    krs = np.concatenate([kr[:, 32:64], kr[:, 0:32]], axis=1)
    w1 = np.concatenate([w[:, 0:640], kr, kr, krs, krs, w[:, 704:1984]], axis=1)
    assert w1.shape[1] == W1COLS
    sh["win1"] = _chunked(w1)
    uq = inp["mla_w_uq"][0].reshape(384, 6, 192)
    nope = uq[:, :, 0:128].reshape(384, 768)
    ropec = uq[:, :, 128:192]
    swp = np.concatenate([ropec[:, :, 32:64], ropec[:, :, 0:32]], axis=2)
    sh["wq"] = _chunked(np.concatenate([nope, ropec.reshape(384, 384), swp.reshape(384, 384)], axis=1))
    ukv = inp["mla_w_ukv"][0].reshape(256, 6, 256)
    sh["wkv"] = _chunked(np.concatenate([ukv[:, :, 0:128].reshape(256, 768), ukv[:, :, 128:256].reshape(256, 768)], axis=1))
    gains = np.zeros((128, NGAIN), f)
    gains[:, 0:8] = _cols(inp["norm_g"][0])
    gains[:, 8:16] = _cols(inp["norm_g"][1])
    gains[:, 16:24] = _cols(inp["final_norm_g"])
    gains[:, 24:32] = _cols(inp["mem_norm_g"])
    gains[:, 32:38] = _cols(inp["pool_scale"][0])
    gains[:, 38:41] = _cols(inp["mla_q_norm_g"][0])
    gains[:, 41:43] = _cols(inp["mla_kv_norm_g"][0])
    sh["gains"] = gains
    cst = np.zeros((128, NCST), f)
    invf = (f(10000.0) ** (-(np.arange(0, 64, 2, dtype=f) / f(64)))).astype(f)
    p = np.arange(128)
    cst[:, 0] = invf[p % 32]
    cst[:, 1] = np.pi / 2
    cst[:, 2] = np.where((p % 64) < 32, np.pi, 0.0)
    for li, w_ in enumerate((2, 4, 8, 16)):
        cst[:, 8 + 16 * li:8 + 16 * li + 16] = 1.0 / np.minimum(np.arange(16) + 1, w_)
    sh["cst"] = cst
    sh["ident"] = np.eye(128, dtype=f)
    NEG = -30000.0
    mneg = np.zeros((128, 2, 256), f)
    mneg[64:128, 0, 0:64] = NEG
    mneg[:, 1, 0:128] = NEG
    mneg[64:128, 1, 128:192] = NEG
    sh["mneg"] = mneg
    return {k: np.ascontiguousarray(v, dtype=f) for k, v in sh.items()}


_NC_CACHE = {}


def kernel(**inp):
    inp = {k: np.asarray(v) for k, v in inp.items()}
    x, mem, positions = inp["x"], inp["mem"], inp["positions"]
    Bt, S, _ = x.shape
    ncores = 8
    NB = Bt // ncores
    sh = prep_shared(inp)
    key = (S, NB)
    if key not in _NC_CACHE:
        _NC_CACHE[key] = build_nc(S=S, NB=NB)
    nc = _NC_CACHE[key]
    in_maps = []
    for c in range(ncores):
        sl = slice(c * NB, (c + 1) * NB)
        m = dict(sh)
        m["xT"] = np.ascontiguousarray(x[sl].transpose(0, 2, 1))
        m["memT"] = np.ascontiguousarray(mem[sl].transpose(0, 2, 1))
        m["pos"] = np.ascontiguousarray(positions[sl].astype(np.int32))
        in_maps.append(m)
    res = run_bass_kernel_spmd(nc, in_maps, core_ids=list(range(ncores)))
    out = np.concatenate([np.asarray(r["outT"]).transpose(0, 2, 1) for r in res.results], axis=0)
    return np.ascontiguousarray(out, dtype=np.float32)
```

```python
import contextlib
import numpy as np
import concourse.bass as bass
import concourse.mybir as mybir
from concourse.bass_utils import run_bass_kernel_spmd

F32 = mybir.dt.float32
BF16 = mybir.dt.bfloat16
I32 = mybir.dt.int32
ACT = mybir.ActivationFunctionType
ALU = mybir.AluOpType

COMPUTE = ("pe", "act", "dve", "pool")
NDMASEM = 8
D = 1024
NMEM = 256
EPS = 1e-6
PI = float(np.pi)


class Op:
    __slots__ = ("eng", "fn", "reads", "writes", "dma", "deps", "sig", "sem", "val", "idx")

    def __init__(self, eng, fn, reads, writes, dma):
        self.eng, self.fn, self.reads, self.writes, self.dma = eng, fn, tuple(reads), tuple(writes), dma
        self.deps = []
        self.sig = False
        self.sem = None
        self.val = None


class Prog:
    def __init__(self):
        self.ops = []
        self.barriers = []

    def op(self, eng, fn, reads=(), writes=()):
        o = Op(eng, fn, reads, writes, False)
        self.ops.append(o)
        return o

    def dma(self, queue, out, in_, reads=(), writes=()):
        o = Op(queue, lambda e: e.dma_start(out=out, in_=in_), reads, writes, True)
        self.ops.append(o)
        return o

    def barrier(self):
        self.barriers.append(len(self.ops))

    def analyze(self):
        last_w = {}
        readers = {}
        bars = set(self.barriers)
        last_on = {}
        dmas_since = []
        pending_bar = {}
        for i, o in enumerate(self.ops):
            o.idx = i
            if i in bars:
                bd = list(last_on.values()) + list(dmas_since)
                dmas_since = []
                for e in COMPUTE + ("sp",):
                    pending_bar[e] = pending_bar.get(e, []) + bd
            deps = {}
            for k in o.reads:
                w = last_w.get(k)
                if w is not None:
                    deps[id(w)] = w
            for k in o.writes:
                w = last_w.get(k)
                if w is not None:
                    deps[id(w)] = w
                for r in readers.get(k, ()):
                    deps[id(r)] = r
            if o.eng in pending_bar:
                for d in pending_bar.pop(o.eng):
                    deps[id(d)] = d
            deps.pop(id(o), None)
            dl = []
            for d in deps.values():
                if (not d.dma) and (not o.dma) and d.eng == "pe" and o.eng == "pe":
                    continue
                dl.append(d)
            o.deps = dl
            for d in dl:
                d.sig = True
            for k in o.reads:
                lst = readers.setdefault(k, [])
                if not o.dma:
                    lst[:] = [r for r in lst if r.dma or r.eng != o.eng]
                lst.append(o)
            for k in o.writes:
                last_w[k] = o
                readers[k] = []
            if o.dma:
                dmas_since.append(o)
            else:
                last_on[o.eng] = o
        cnt = {e: 0 for e in COMPUTE}
        dcnt = {}
        self.final = {}
        for o in self.ops:
            if o.dma:
                j = dcnt.get(o.eng, 0)
                dcnt[o.eng] = j + 1
                o.sem = ("dma", o.eng, j % NDMASEM)
                o.val = 16 * (j // NDMASEM + 1)
                self.final[o.sem] = o.val
            elif o.sig:
                cnt[o.eng] += 1
                o.sem = ("eng", o.eng)
                o.val = cnt[o.eng]

    def sem_keys(self):
        ks = [("eng", e) for e in COMPUTE]
        for q in sorted({o.eng for o in self.ops if o.dma}):
            for j in range(NDMASEM):
                ks.append(("dma", q, j))
        return ks

    def emit(self, block, sems):
        per = {}
        for o in self.ops:
            per.setdefault(o.eng, []).append(o)
        final = self.final

        def run(engname, eng):
            seen = {}
            for o in per.get(engname, ()):
                need = {}
                for d in o.deps:
                    if need.get(d.sem, 0) < d.val:
                        need[d.sem] = d.val
                if o.dma and o.val > 16 and need.get(o.sem, 0) < o.val - 16:
                    need[o.sem] = o.val - 16
                for s, v in need.items():
                    if seen.get(s, 0) < v:
                        eng.wait_ge(sems[s], v)
                        seen[s] = v
                ins = o.fn(eng)
                if o.dma:
                    ins.then_inc(sems[o.sem], 16)
                elif o.sig:
                    ins.then_inc(sems[o.sem], 1)
            if engname == "sp":
                for s, v in final.items():
                    eng.wait_ge(sems[s], v)

        @block.tensor
        def _(e):
            run("pe", e)

        @block.scalar
        def _(e):
            run("act", e)

        @block.vector
        def _(e):
            run("dve", e)

        @block.gpsimd
        def _(e):
            run("pool", e)

        @block.sync
        def _(e):
            run("sp", e)


W1COLS = 384 + 256 + 128 + 128 + 256 + 1024
WQCOLS = 768 + 384 + 384
NGAIN = 8 + 8 + 8 + 8 + 6 + 3 + 2
NCST = 8 + 64


def build_nc(S=2048, NB=2, TT=256, phases=(0, 1)):
    NT = S // TT
    CPT = TT // 128
    nc = bass.Bass("TRN2", target_bir_lowering=False)
    P = Prog()

    def din(name, shape, dt=F32):
        return nc.dram_tensor(name, list(shape), dt, kind="ExternalInput").ap()

    xT = din("xT", [NB, D, S])
    memT = din("memT", [NB, D, NMEM])
    pos = din("pos", [NB, S], I32)
    gains_d = din("gains", [128, NGAIN])
    cst_d = din("cst", [128, NCST])
    ident_d = din("ident", [128, 128])
    mneg_d = din("mneg", [128, 2, 256])
    win0_d = din("win0", [128, 8, 2048])
    wout0_d = din("wout0", [128, 8, 1024])
    wbd_d = din("wbd", [128, 6, 768])
    wmkv0_d = din("wmkv0", [128, 8, 512])
    win1_d = din("win1", [128, 8, W1COLS])
    wout1_d = din("wout1", [128, 8, 1024])
    wq_d = din("wq", [128, 3, WQCOLS])
    wkv_d = din("wkv", [128, 2, 1536])
    wmkv1_d = din("wmkv1", [128, 8, 512])
    outT = nc.dram_tensor("outT", [NB, D, S], F32, kind="ExternalOutput").ap()
    if 0 in phases and 1 in phases:
        h1 = nc.dram_tensor("h1", [NB, D, S], F32).ap()
    elif 0 in phases:
        h1 = nc.dram_tensor("h1", [NB, D, S], F32, kind="ExternalOutput").ap()
    else:
        h1 = din("h1", [NB, D, S])

    ARENA0, ARENA1 = 16384 + 512, 229376 - 256
    cur = [ARENA0]
    hi = [ARENA0]

    uniq = [0]

    def sb(name, shape, dt):
        uniq[0] += 1
        name = "%s_%d" % (name, uniq[0])
        esz = 2 if dt == BF16 else 4
        n = int(np.prod(shape[1:])) * esz
        n = (n + 63) // 64 * 64
        t = nc.alloc_sbuf_tensor_at(name, list(shape), dt, offset=cur[0])
        cur[0] += n
        hi[0] = max(hi[0], cur[0])
        assert cur[0] <= ARENA1, ("SBUF overflow", name, cur[0])
        return t

    ps = [nc.alloc_psum_tensor("ps%d" % i, [128, 512], F32) for i in range(8)]
    rot = {"proj": [0, (0, 1)], "sc": [0, (2, 3)], "ao": [0, (4, 5)], "ad": [0, (6, 7)]}

    def bank(kind):
        r = rot[kind]
        b = r[1][r[0] % len(r[1])]
        r[0] += 1
        return b

    def mm(bk, n, lhsT, rhs, start, stop, reads):
        P.op("pe", lambda e: e.matmul(ps[bk][:, 0:n], lhsT, rhs, start=start, stop=stop), reads=reads, writes=[("ps", bk)])

    def mmo(out, bk, lhsT, rhs, start, stop, reads):
        P.op("pe", lambda e: e.matmul(out, lhsT, rhs, start=start, stop=stop), reads=reads, writes=[("ps", bk)])

    def act(out, in_, func, reads, writes, **kw):
        P.op("act", lambda e: e.activation(out, in_, func, **kw), reads=reads, writes=writes)

    def copy(eng, out, in_, reads, writes):
        if eng == "act":
            P.op("act", lambda e: e.activation(out, in_, ACT.Copy), reads=reads, writes=writes)
        else:
            P.op(eng, lambda e: e.tensor_copy(out, in_), reads=reads, writes=writes)

    def tt(eng, out, a, b, op, reads, writes):
        P.op(eng, lambda e: e.tensor_tensor(out, a, b, op), reads=reads, writes=writes)

    def ts(eng, out, a, s1, s2, op0, op1, reads, writes):
        if op1 is None:
            P.op(eng, lambda e: e.tensor_scalar(out, a, s1, None, op0), reads=reads, writes=writes)
        else:
            P.op(eng, lambda e: e.tensor_scalar(out, a, s1, s2, op0, op1), reads=reads, writes=writes)

    def stt(eng, out, a, s, b, op0, op1, reads, writes):
        P.op(eng, lambda e: e.scalar_tensor_tensor(out, a, s, b, op0, op1), reads=reads, writes=writes)

    def memset(eng, ap, v, writes):
        P.op(eng, lambda e: e.memset(ap, v), writes=writes)

    gains = sb("gains", [128, NGAIN], F32)
    cst = sb("cst", [128, NCST], F32)
    ones = sb("ones", [128, 128], BF16)
    ind = [sb("ind%d" % r, [128, 128], BF16) for r in range(2)]
    G0, G1, GF, GM, PSC, GQ, GKV = 0, 8, 16, 24, 32, 38, 41
    ident = sb("ident", [128, 128], BF16)
    mneg = sb("mneg", [128, 2, 256], BF16)
    P.dma("pool", ident[:, :], ident_d, writes=["ident"])
    P.dma("pool", mneg[:, :, :], mneg_d, writes=["mneg"])
    P.dma("sp", gains[:, :], gains_d, writes=["gains"])
    P.dma("sp", cst[:, :], cst_d, writes=["cst"])
    memset("pool", ones[:, :], 1.0, ["ones"])
    for r in range(2):
        memset("pool", ind[r][:, :], 0.0, [("ind", r)])
        memset("pool", ind[r][:, r * 64:(r + 1) * 64], 1.0, [("ind", r)])

    if 1 in phases:
        win1 = sb("win1", [128, 8, W1COLS], BF16)
        wq = sb("wq", [128, 3, WQCOLS], BF16)
        wkv = sb("wkv", [128, 2, 1536], BF16)
        wmkv1 = sb("wmkv1", [128, 8, 512], BF16)
    shared0 = cur[0]

    def load_w(t, d, nch, key):
        for c in range(nch):
            P.dma("pool", t[:, c, :], d[:, c, :], writes=[(key, c)])

    def load_l1_weights():
        load_w(wmkv1, wmkv1_d, 8, "wmkv1")
        load_w(win1, win1_d, 8, "win1")
        load_w(wq, wq_d, 3, "wq")
        load_w(wkv, wkv_d, 2, "wkv")

    def rms_stats(B, src, srckeys, nch, n, ncols, tag):
        bk = bank("proj")
        for c in range(nch):
            sq = B["sq"][c % 2]
            act(sq[:, 0:ncols], src(c), ACT.Square, reads=[srckeys(c)], writes=[("sq", c % 2)])
            mm(bk, ncols, ones[:, :], sq[:, 0:ncols], c == 0, c == nch - 1, reads=["ones", ("sq", c % 2)])
        rstd = B["rstd" + tag]
        rk = "rstd" + tag
        act(rstd[:, 0:ncols], ps[bk][:, 0:ncols], ACT.Ln, reads=[("ps", bk)], writes=[rk], bias=EPS, scale=1.0 / n)
        act(rstd[:, 0:ncols], rstd[:, 0:ncols], ACT.Exp, reads=[rk], writes=[rk], scale=-0.5)

    def mem_prep(B, b, wmkv, wkey):
        memKm, memVp = B["memKm"][b], B["memVp"][b]
        memn = B["br"]
        mem32 = B["X"][1]
        assert TT == NMEM
        P.dma("sp", mem32[:, :, :], memT[b].rearrange("(c p) t -> p c t", p=128), writes=[("X1", c) for c in range(8)])
        rms_stats(B, lambda c: mem32[:, c, :], lambda c: ("X1", c), 8, D, NMEM, "h")
        for c in range(8):
            stt("dve", memn[:, c, :], mem32[:, c, :], gains[:, GM + c:GM + c + 1], B["rstdh"][:, 0:NMEM], ALU.mult, ALU.mult,
                reads=[("X1", c), "gains", "rstdh"], writes=[("br", c)])
        for j in range(2):
            bk = bank("proj")
            for c in range(8):
                mm(bk, NMEM, wmkv[:, c, j * 128:(j + 1) * 128], memn[:, c, :], c == 0, c == 7, reads=[(wkey, c), ("br", c)])
            copy("act", memKm[0][0:64, j, :], ps[bk][0:64, 0:NMEM], reads=[("ps", bk)], writes=[("memKm", b, 0)])
            copy("dve", memKm[1][64:128, j, :], ps[bk][64:128, 0:NMEM], reads=[("ps", bk)], writes=[("memKm", b, 1)])
        for kc in range(2):
            bk = bank("proj")
            for c in range(8):
                mm(bk, 256, memn[:, c, kc * 128:(kc + 1) * 128], wmkv[:, c, 256:512], c == 0, c == 7, reads=[(wkey, c), ("br", c)])
            src = ps[bk][:, 0:256].rearrange("p (j r d) -> p j r d", j=2, r=2)
            for r in range(2):
                copy("act" if r == 0 else "dve", memVp[:, kc, :, r, r * 64:(r + 1) * 64], src[:, :, r, :],
                     reads=[("ps", bk)], writes=[("memVp", b)])

    def finish_head(B, bo, bd, c, par):
        rden, rs = B["rden"], B["rs"]
        k = B["rr"][0] % 2
        B["rr"][0] += 1
        act(rden[k][:, :], ps[bd][:, 0:TT], ACT.Ln, reads=[("ps", bd)], writes=[("rden", k)])
        act(rden[k][:, :], rden[k][:, :], ACT.Exp, reads=[("rden", k)], writes=[("rden", k)], scale=-1.0)
        tt("dve", rs[k][:, :], rden[k][:, :], B["sg"][par][:, c, :], ALU.mult, reads=[("rden", k), ("sg", par, c)], writes=[("rs", k)])
        tt("dve", B["br"][:, c, :], ps[bo][:, 0:TT], rs[k][:, :], ALU.mult, reads=[("ps", bo), ("rs", k)], writes=[("br", c)])

    def next_e(B):
        k = B["er"][0] % 3
        B["er"][0] += 1
        return k

    def mem_attn_units(B, b, par):
        units = []
        mq, memKm, memVp = B["mq"][par], B["memKm"][b], B["memVp"][b]
        for j in range(2):
            def u(j=j):
                bo, bd = bank("ao"), bank("ad")

                def score(r):
                    sbk = bank("sc")
                    for kc in range(2):
                        mmo(ps[sbk][:, kc * TT:(kc + 1) * TT], sbk, memKm[r][:, j, kc * 128:(kc + 1) * 128], mq[:, j, :], True, True,
                            reads=[("memKm", b, r), ("mq", par, j)])
                    return sbk

                pend = score(0)
                n = 0
                for r in range(2):
                    sbk = pend
                    if r == 0:
                        pend = score(1)
                    ek = next_e(B)
                    E = B["E"][ek]
                    act(E[:, :], ps[sbk][:, 0:2 * TT], ACT.Exp, reads=[("ps", sbk)], writes=[("E", ek)], scale=0.125)
                    for kc in range(2):
                        mm(bo, TT, memVp[:, kc, j, r, :], E[:, kc * TT:(kc + 1) * TT], n == 0, n == 3, reads=[("memVp", b), ("E", ek)])
                        mm(bd, TT, ind[r][:, :], E[:, kc * TT:(kc + 1) * TT], n == 0, n == 3, reads=[("ind", r), ("E", ek)])
                        n += 1
                finish_head(B, bo, bd, 6 + j, par)
            units.append((12, u))
        return units

    def out_proj_units(B, X, xk, wout, wkey):
        units = []
        for co in range(8):
            def u(co=co):
                bk = bank("proj")
                for ci in range(8):
                    mm(bk, TT, wout[:, ci, co * 128:(co + 1) * 128], B["br"][:, ci, :], ci == 0, ci == 7, reads=[(wkey, ci), ("br", ci)])
                tt("dve", X[:, co, :], X[:, co, :], ps[bk][:, 0:TT], ALU.add, reads=[(xk, co), ("ps", bk)], writes=[(xk, co)])
            units.append((8, u))
        return units

    def norm_in(B, X, xk, goff):
        rms_stats(B, lambda c: X[:, c, :], lambda c: (xk, c), 8, D, TT, "h")
        for c in range(8):
            stt("dve", B["hn"][:, c, :], X[:, c, :], gains[:, goff + c:goff + c + 1], B["rstdh"][:, :], ALU.mult, ALU.mult,
                reads=[(xk, c), "gains", "rstdh"], writes=[("hn", c)])

    def common_bufs(ph):
        B = {}
        B["sq"] = [sb("sq%d" % k, [128, TT], BF16) for k in range(2)]
        for tg in (("h",) if ph == 0 else ("h", "q", "kv")):
            B["rstd" + tg] = sb("rstd" + tg, [128, TT], F32)
        B["hn"] = sb("hn", [128, 8, TT], BF16)
        B["mq"] = [sb("mq%d" % k, [128, 2, TT], BF16) for k in range(2)]
        B["sg"] = [sb("sg%d" % k, [128, 8, TT], BF16) for k in range(2)]
        B["br"] = sb("br", [128, 8, TT], BF16)
        B["E"] = [sb("E%d" % k, [128, 2 * TT], BF16) for k in range(3)]
        B["er"] = [0]
        B["rr"] = [0]
        B["rden"] = [sb("rden%d" % k, [128, TT], F32) for k in range(2)]
        B["rs"] = [sb("rs%d" % k, [128, TT], F32) for k in range(2)]
        B["memKm"] = [[sb("memKm%d_%d" % (b, r), [128, 2, NMEM], BF16) for r in range(2)] for b in range(NB)]
        B["memVp"] = [sb("memVp%d" % b, [128, 2, 2, 2, 128], BF16) for b in range(NB)]
        B["X"] = [sb("X%d" % k, [128, 8, TT], F32) for k in range(3 if ph == 0 else 2)]
        for b in range(NB):
            for r in range(2):
                memset("pool", B["memKm"][b][r][:, :, :], 0.0, [("memKm", b, r)])
            memset("pool", B["memVp"][b][:, :, :, :, :], 0.0, [("memVp", b)])
        return B

    def run_units(units):
        for _, f in units:
            f()

    def merge_units(ua, ub, A_OFFSET):
        items = []
        for lst, tag in ((ua, 0), (ub, 1)):
            tot = float(sum(w for w, _ in lst)) or 1.0
            acc = 0.0
            off = A_OFFSET if tag == 0 else 0.0
            for w, f in lst:
                items.append((off + (1.0 - off) * (acc + 0.5 * w) / tot, tag, f))
                acc += w
        items.sort(key=lambda t: (t[0], t[1]))
        for _, _, f in items:
            f()

    def pipeline(stageA, stageB, ntiles, overlap_ok=lambda n: True, a_off=0.15):
        import os
        if os.environ.get("NOPIPE"):
            overlap_ok = lambda n: False
        run_units(stageA(0))
        for n in range(ntiles):
            ub = stageB(n)
            if n + 1 < ntiles and overlap_ok(n + 1):
                merge_units(stageA(n + 1), ub, a_off)
            else:
                run_units(ub)
                if n + 1 < ntiles:
                    run_units(stageA(n + 1))

    NTILES = NB * NT
    A_OFFSET = 0.15

    if 0 in phases:
        cur[0] = shared0
        win0 = sb("win0", [128, 8, 2048], BF16)
        wout0 = sb("wout0", [128, 8, 1024], BF16)
        wbd = sb("wbd", [128, 6, 768], BF16)
        wmkv0 = sb("wmkv0", [128, 8, 512], BF16)
        load_w(wmkv0, wmkv0_d, 8, "wmkv0")
        load_w(win0, win0_d, 8, "win0")
        load_w(wbd, wbd_d, 6, "wbd")
        load_w(wout0, wout0_d, 8, "wout0")
        B = common_bufs(0)
        UW = 16 + TT
        ub_ = [sb("u%d" % c, [128, UW], F32) for c in range(6)]
        tp = [[sb("tp%d_%d" % (k, l), [128, UW], F32) for l in range(2)] for k in range(3)]
        mixed = [sb("mixed%d" % k, [128, 6, TT], BF16) for k in range(2)]
        fix = sb("fix", [128, 16], F32)
        SEG = {0: [(0, 128, 1)], 1: [(0, 64, 1), (64, 128, 2)], 2: [(0, 128, 2)],
               3: [(0, 128, 3)], 4: [(0, 64, 3), (64, 128, 4)], 5: [(0, 128, 4)]}
        GL = []
        for co in range(6):
            gs = {(128 * co) // 192, (128 * co + 127) // 192}
            cis = sorted({ci for g in gs for ci in range((192 * g) // 128, (192 * g + 191) // 128 + 1)})
            GL.append(cis)

        def load_x(n):
            if n >= NTILES:
                return
            b_, i_ = divmod(n, NT)
            P.dma("sp", B["X"][n % 3][:, :, :], xT[b_].rearrange("(c p) t -> p c t", p=128)[:, :, i_ * TT:(i_ + 1) * TT],
                  writes=[("X%d" % (n % 3), c) for c in range(8)])

        for b in range(NB):
            mem_prep(B, b, wmkv0, "wmkv0")
        load_x(0)
        load_x(1)

        def stageA0(n):
            b, i = divmod(n, NT)
            par = n % 2
            X, xk = B["X"][n % 3], "X%d" % (n % 3)
            units = []

            def u_pre():
                if n >= 1:
                    load_x(n + 1)
                if n == 1 and 1 in phases:
                    load_l1_weights()
                if i == 0:
                    for c in range(6):
                        memset("pool", ub_[c][:, 0:16], 0.0, [("u", c)])
            units.append((0, u_pre))
            pend_fin = {}
            for gi in range(16):
                def u(gi=gi):
                    bk = bank("proj")
                    for c in range(8):
                        mm(bk, TT, win0[:, c, gi * 128:(gi + 1) * 128], B["hn"][:, c, :], c == 0, c == 7, reads=[("win0", c), ("hn", c)])
                    if gi < 6:
                        copy("act", ub_[gi][:, 16:UW], ps[bk][:, 0:TT], reads=[("ps", bk)], writes=[("u", gi)])
                    elif gi < 8:
                        copy("dve", B["mq"][par][:, gi - 6, :], ps[bk][:, 0:TT], reads=[("ps", bk)], writes=[("mq", par, gi - 6)])
                    else:
                        copy("dve" if gi % 2 else "act", B["sg"][par][:, gi - 8, :], ps[bk][:, 0:TT], reads=[("ps", bk)], writes=[("sg", par, gi - 8)])
                    if 0 <= gi - 3 < 6:
                        pool_final(gi - 3)
                    if gi < 6:
                        pool_adds(gi)
                units.append((8, u))

            def u_silu():
                for j in range(8):
                    act(B["sg"][par][:, j, :], B["sg"][par][:, j, :], ACT.Silu, reads=[("sg", par, j)], writes=[("sg", par, j)])
            units.append((1, u_silu))

            def pool_adds(c):
                k = c % 3
                U = ub_[c]
                segs = []
                for (r0, r1, L) in SEG[c]:
                    srcb, srck = U, [("u", c)]
                    for l in range(1, L + 1):
                        dst = tp[k][(l - 1) % 2]
                        dk = [("tp", k, (l - 1) % 2, hh) for hh in (0, 64) if r0 <= hh < r1]
                        lo = (1 << l) - 1
                        sh = 1 << (l - 1)
                        tt("pool", dst[r0:r1, lo:UW], srcb[r0:r1, lo:UW], srcb[r0:r1, lo - sh:UW - sh], ALU.add,
                           reads=srck, writes=dk)
                        srcb, srck = dst, dk
                    segs.append((r0, r1, L, srcb, srck))
                pend_fin[c] = segs

            def pool_final(c):
                mx = mixed[par]
                U = ub_[c]
                for (r0, r1, L, srcb, srck) in pend_fin[c]:
                    w = 1 << L
                    stt("dve", mx[r0:r1, c, :], srcb[r0:r1, 16:UW], 1.0 / w, U[r0:r1, 16:UW], ALU.mult, ALU.subtract,
                        reads=srck + [("u", c)], writes=[("mixed", par, c)])
                    if i == 0:
                        tt("pool", fix[r0:r1, :], srcb[r0:r1, 16:32], cst[r0:r1, 8 + 16 * (L - 1):8 + 16 * L], ALU.mult,
                           reads=srck + ["cst"], writes=["fix"])
                        tt("pool", mx[r0:r1, c, 0:16], fix[r0:r1, :], U[r0:r1, 16:32], ALU.subtract,
                           reads=["fix", ("u", c)], writes=[("mixed", par, c)])
                copy("pool", U[:, 0:16], U[:, TT:UW], reads=[("u", c)], writes=[("u", c)])

            def u_norm_next():
                if n + 1 < NTILES:
                    norm_in(B, B["X"][(n + 1) % 3], "X%d" % ((n + 1) % 3), G0)
            units.append((8, u_norm_next))
            return units

        def stageB0(n):
            b, i = divmod(n, NT)
            par = n % 2
            X, xk = B["X"][n % 3], "X%d" % (n % 3)
            units = []
            for co in range(6):
                def u(co=co):
                    bk = bank("proj")
                    cis = GL[co]
                    for k_, ci in enumerate(cis):
                        mm(bk, TT, wbd[:, ci, co * 128:(co + 1) * 128], mixed[par][:, ci, :], k_ == 0, k_ == len(cis) - 1,
                           reads=[("wbd", ci), ("mixed", par, ci)])
                    stt("dve", B["br"][:, co, :], ps[bk][:, 0:TT], gains[:, PSC + co:PSC + co + 1], B["sg"][par][:, co, :], ALU.mult, ALU.mult,
                        reads=[("ps", bk), "gains", ("sg", par, co)], writes=[("br", co)])
                units.append((2, u))
            units += mem_attn_units(B, b, par)
            units += out_proj_units(B, X, xk, wout0, "wout0")

            def u_store():
                P.dma("sp", h1[b].rearrange("(c p) t -> p c t", p=128)[:, :, i * TT:(i + 1) * TT], X[:, :, :],
                      reads=[(xk, c) for c in range(8)], writes=[("h1", b, i)])
            units.append((0, u_store))
            return units

        nc._p0_end = cur[0]
        norm_in(B, B["X"][0], "X0", G0)
        pipeline(stageA0, stageB0, NTILES, a_off=0.0)
        P.barrier()

    if 1 in phases:
        cur[0] = shared0
        if 0 not in phases:
            load_l1_weights()
        wout1 = sb("wout1", [128, 8, 1024], BF16)
        load_w(wout1, wout1_d, 8, "wout1")
        B = common_bufs(1)
        KnT = sb("KnT", [128, 6, S], BF16)
        krT = [sb("krT%d" % r, [128, S], BF16) for r in range(2)]
        V = sb("V", [128, S // 128, 768], BF16)
        cq = sb("cq", [128, 3, TT], BF16)
        ckv = sb("ckv", [128, 2, TT], BF16)
        cqn = sb("cqn", [128, 3, TT], BF16)
        ckvn = sb("ckvn", [128, 2, TT], BF16)
        qn = [sb("qn%d" % k, [128, 6, TT], BF16) for k in range(2)]
        qr = [sb("qr%d" % k, [128, 3, TT], BF16) for k in range(2)]
        posf = sb("posf", [128, TT], F32)
        ang = sb("ang", [128, TT], F32)
        ang2 = sb("ang2", [128, TT], F32)
        cs = sb("cs", [128, TT], F32)
        sn = sb("sn", [128, TT], F32)
        rt = [sb("rt%d" % k, [128, TT], F32) for k in range(2)]
        posi = ang[:, :].bitcast(I32)
        ki = rt[1][:, :].bitcast(I32)
        kf = rt[0]
        for r in range(2):
            memset("pool", krT[r][:, :], 0.0, [("kr", r, i) for i in range(NT)])
        SCALE = float(192.0 ** -0.5)

        def trig_arg(a, akey, phase_col):
            ts("dve", a[:, :], posf[:, :], cst[:, 0:1], cst[:, phase_col:phase_col + 1], ALU.mult, ALU.add,
               reads=["posf", "cst"], writes=[akey])
            ts("dve", ki, a[:, :], 1.0 / (2 * PI), None, ALU.mult, None, reads=[akey], writes=[("rt", 1)])
            copy("dve", kf[:, :], ki, reads=[("rt", 1)], writes=[("rt", 0)])
            stt("dve", a[:, :], kf[:, :], -2 * PI, a[:, :], ALU.mult, ALU.add, reads=[("rt", 0), akey], writes=[akey])
            ts("dve", a[:, :], a[:, :], -PI, PI, ALU.max, ALU.min, reads=[akey], writes=[akey])

        def rope_mul(bk_raw, bk_swp):
            tt("dve", rt[0][:, :], ps[bk_raw][:, 0:TT], cs[:, :], ALU.mult, reads=[("ps", bk_raw), "cs"], writes=[("rt", 0)])
            tt("dve", rt[1][:, :], ps[bk_swp][:, 0:TT], sn[:, :], ALU.mult, reads=[("ps", bk_swp), "sn"], writes=[("rt", 1)])

        def load_h(n):
            if n >= NTILES:
                return
            b_, i_ = divmod(n, NT)
            P.dma("sp", B["X"][n % 2][:, :, :], h1[b_].rearrange("(c p) t -> p c t", p=128)[:, :, i_ * TT:(i_ + 1) * TT],
                  reads=[("h1", b_, i_)], writes=[("X%d" % (n % 2), c) for c in range(8)])

        for b in range(NB):
            mem_prep(B, b, wmkv1, "wmkv1")
        load_h(0)
        load_h(1)

        def proj8(col, hn):
            bk = bank("proj")
            for c in range(8):
                mm(bk, TT, win1[:, c, col:col + 128], hn[:, c, :], c == 0, c == 7, reads=[("win1", c), ("hn", c)])
            return bk

        def stageA1(n):
            b, i = divmod(n, NT)
            par = n % 2
            t0 = i * TT
            X, xk = B["X"][par], "X%d" % par
            hn = B["hn"]
            units = []

            def u_norm():
                P.dma("sp", posi, pos[b:b + 1, t0:t0 + TT].partition_broadcast(128), writes=["ang"])
                copy("dve", posf[:, :], posi, reads=["ang"], writes=["posf"])
                trig_arg(ang, "ang", 1)
                trig_arg(ang2, "ang2", 2)
                norm_in(B, X, xk, G1)
            units.append((24, u_norm))
            for j in range(8):
                def u(j=j):
                    bk = proj8(1152 + 128 * j, hn)
                    copy("dve", B["sg"][par][:, j, :], ps[bk][:, 0:TT], reads=[("ps", bk)], writes=[("sg", par, j)])
                units.append((8, u))

            def u_silu_block():
                for j in range(8):
                    act(B["sg"][par][:, j, :], B["sg"][par][:, j, :], ACT.Silu, reads=[("sg", par, j)], writes=[("sg", par, j)])
                act(cs[:, :], ang[:, :], ACT.Sin, reads=["ang"], writes=["cs"])
                act(sn[:, :], ang2[:, :], ACT.Sin, reads=["ang2"], writes=["sn"])
                for j in range(3):
                    bk = proj8(128 * j, hn)
                    copy("dve", cq[:, j, :], ps[bk][:, 0:TT], reads=[("ps", bk)], writes=[("cq", j)])
                for j in range(2):
                    bk = proj8(384 + 128 * j, hn)
                    copy("dve", ckv[:, j, :], ps[bk][:, 0:TT], reads=[("ps", bk)], writes=[("ckv", j)])
                for j in range(2):
                    bk = proj8(896 + 128 * j, hn)
                    copy("dve", B["mq"][par][:, j, :], ps[bk][:, 0:TT], reads=[("ps", bk)], writes=[("mq", par, j)])
            units.append((56, u_silu_block))

            def u_lat():
                rms_stats(B, lambda c: cq[:, c, :], lambda c: ("cq", c), 3, 384, TT, "q")
                for j in range(3):
                    stt("dve", cqn[:, j, :], cq[:, j, :], gains[:, GQ + j:GQ + j + 1], B["rstdq"][:, :], ALU.mult, ALU.mult,
                        reads=[("cq", j), "gains", "rstdq"], writes=[("cqn", j)])
                rms_stats(B, lambda c: ckv[:, c, :], lambda c: ("ckv", c), 2, 256, TT, "kv")
                for j in range(2):
                    stt("dve", ckvn[:, j, :], ckv[:, j, :], gains[:, GKV + j:GKV + j + 1], B["rstdkv"][:, :], ALU.mult, ALU.mult,
                        reads=[("ckv", j), "gains", "rstdkv"], writes=[("ckvn", j)])
            units.append((5, u_lat))

            def u_kr():
                bkr = proj8(640, hn)
                bks = proj8(768, hn)
                rope_mul(bkr, bks)
                tt("pool", krT[0][0:64, t0:t0 + TT], rt[0][0:64, :], rt[1][0:64, :], ALU.add,
                   reads=[("rt", 0), ("rt", 1)], writes=[("kr", 0, i)])
                tt("pool", krT[1][64:128, t0:t0 + TT], rt[0][64:128, :], rt[1][64:128, :], ALU.add,
                   reads=[("rt", 0), ("rt", 1)], writes=[("kr", 1, i)])
            units.append((16, u_kr))
            for h in range(6):
                def u(h=h):
                    bk = bank("proj")
                    for c in range(2):
                        mm(bk, TT, wkv[:, c, h * 128:(h + 1) * 128], ckvn[:, c, :], c == 0, c == 1, reads=[("wkv", c), ("ckvn", c)])
                    copy("act" if h % 2 == 0 else "dve", KnT[:, h, t0:t0 + TT], ps[bk][:, 0:TT], reads=[("ps", bk)], writes=[("kn", h, i)])
                units.append((2, u))
            for s_ in range(CPT):
                def u(s_=s_):
                    kc = i * CPT + s_
                    for (o0, ncol) in ((0, 512), (512, 256)):
                        bk = bank("proj")
                        for c in range(2):
                            mm(bk, ncol, ckvn[:, c, s_ * 128:(s_ + 1) * 128], wkv[:, c, 768 + o0:768 + o0 + ncol], c == 0, c == 1,
                               reads=[("wkv", c), ("ckvn", c)])
                        copy("act" if o0 == 0 else "dve", V[:, kc, o0:o0 + ncol], ps[bk][:, 0:ncol], reads=[("ps", bk)], writes=[("V", kc)])
                units.append((6, u))
            for h in range(6):
                def u(h=h):
                    bk = bank("proj")
                    for c in range(3):
                        mm(bk, TT, wq[:, c, h * 128:(h + 1) * 128], cqn[:, c, :], c == 0, c == 2, reads=[("wq", c), ("cqn", c)])
                    copy("act" if h % 2 == 0 else "dve", qn[par][:, h, :], ps[bk][:, 0:TT], reads=[("ps", bk)], writes=[("qn", par, h)])
                units.append((3, u))
            for j in range(3):
                def u(j=j):
                    bkr = bank("proj")
                    for c in range(3):
                        mm(bkr, TT, wq[:, c, 768 + j * 128:768 + (j + 1) * 128], cqn[:, c, :], c == 0, c == 2, reads=[("wq", c), ("cqn", c)])
                    bks = bank("proj")
                    for c in range(3):
                        mm(bks, TT, wq[:, c, 1152 + j * 128:1152 + (j + 1) * 128], cqn[:, c, :], c == 0, c == 2, reads=[("wq", c), ("cqn", c)])
                    rope_mul(bkr, bks)
                    tt("pool", qr[par][:, j, :], rt[0][:, :], rt[1][:, :], ALU.add, reads=[("rt", 0), ("rt", 1)], writes=[("qr", par, j)])
                units.append((6, u))
            return units

        def stageB1(n):
            b, i = divmod(n, NT)
            par = n % 2
            t0 = i * TT
            X, xk = B["X"][par], "X%d" % par
            units = []
            for h in range(6):
                def u(h=h):
                    bo, bd = bank("ao"), bank("ad")
                    chunks = [("d", c_) for c_ in range(CPT)] + [("f", kc) for kc in range(i * CPT)]
                    assert CPT == 2
                    pairs = [chunks[k:k + 2] for k in range(0, len(chunks), 2)]
                    npair = len(pairs)

                    def score(pidx):
                        sbk = bank("sc")
                        info = []
                        for half, (kind, v) in enumerate(pairs[pidx]):
                            kc = i * CPT + v if kind == "d" else v
                            ti = kc // CPT
                            out = ps[sbk][:, half * TT:(half + 1) * TT]
                            diag = kind == "d"
                            mmo(out, sbk, KnT[:, h, kc * 128:(kc + 1) * 128], qn[par][:, h, :], True, False,
                                reads=[("kn", h, ti), ("qn", par, h)])
                            mmo(out, sbk, krT[h % 2][:, kc * 128:(kc + 1) * 128], qr[par][:, h // 2, :], False, not diag,
                                reads=[("kr", h % 2, ti), ("qr", par, h // 2)])
                            if diag:
                                mmo(out, sbk, ident[:, :], mneg[:, v, :], False, True, reads=["ident", "mneg"])
                            info.append(kc)
                        return sbk, info

                    pend = score(0)
                    nmm = 0
                    for pidx in range(npair):
                        sbk, info = pend
                        if pidx + 1 < npair:
                            pend = score(pidx + 1)
                        ek = next_e(B)
                        E = B["E"][ek]
                        act(E[:, :], ps[sbk][:, 0:2 * TT], ACT.Exp, reads=[("ps", sbk)], writes=[("E", ek)], scale=SCALE)
                        for half, kc in enumerate(info):
                            first, last = nmm == 0, nmm == 2 * npair - 1
                            mm(bo, TT, V[:, kc, h * 128:(h + 1) * 128], E[:, half * TT:(half + 1) * TT], first, last, reads=[("V", kc), ("E", ek)])
                            mm(bd, TT, ones[:, :], E[:, half * TT:(half + 1) * TT], first, last, reads=["ones", ("E", ek)])
                            nmm += 1
                    finish_head(B, bo, bd, h, par)
                units.append((4 * (i * CPT + CPT), u))
            units += mem_attn_units(B, b, par)
            units += out_proj_units(B, X, xk, wout1, "wout1")

            def u_final():
                rms_stats(B, lambda c: X[:, c, :], lambda c: (xk, c), 8, D, TT, "h")
                for c in range(8):
                    stt("dve", X[:, c, :], X[:, c, :], gains[:, GF + c:GF + c + 1], B["rstdh"][:, :], ALU.mult, ALU.mult,
                        reads=[(xk, c), "gains", "rstdh"], writes=[(xk, c)])
                P.dma("sp", outT[b].rearrange("(c p) t -> p c t", p=128)[:, :, t0:t0 + TT], X[:, :, :],
                      reads=[(xk, c) for c in range(8)])
                load_h(n + 2)
            units.append((8, u_final))
            return units

        nc._p1_end = cur[0]
        pipeline(stageA1, stageB1, NTILES, overlap_ok=lambda n: n % NT != 0, a_off=0.25)

    P.analyze()
    with contextlib.ExitStack() as st:
        sems = {k: st.enter_context(nc.semaphore("s_" + "_".join(map(str, k)))) for k in P.sem_keys()}
        block = st.enter_context(nc.Block())
        P.emit(block, sems)
    nc._n_ops = len(P.ops)
    nc._sbuf_hi = hi[0]
    return nc


def _chunked(w):
    k, n = w.shape
    return np.ascontiguousarray(w.reshape(k // 128, 128, n).transpose(1, 0, 2))


def _cols(v):
    return v.reshape(-1, 128).T


def prep_shared(inp):
    f = np.float32
    sh = {}
    sh["win0"] = _chunked(inp["pool_w_in"][0])
    sh["wout0"] = _chunked(inp["w_out"][0])
    sh["wout1"] = _chunked(inp["w_out"][1])
    sh["wmkv0"] = _chunked(inp["w_mem_kv"][0])
    sh["wmkv1"] = _chunked(inp["w_mem_kv"][1])
    wbd = np.zeros((768, 768), f)
    for g in range(4):
        wbd[192 * g:192 * g + 192, 192 * g:192 * g + 192] = inp["pool_w_group"][0, g]
    sh["wbd"] = _chunked(wbd)
    w = inp["mla_w_in"][0]
    kr = w[:, 640:704]
    krs = np.concatenate([kr[:, 32:64], kr[:, 0:32]], axis=1)
    w1 = np.concatenate([w[:, 0:640], kr, kr, krs, krs, w[:, 704:1984]], axis=1)
    assert w1.shape[1] == W1COLS
    sh["win1"] = _chunked(w1)
    uq = inp["mla_w_uq"][0].reshape(384, 6, 192)
    nope = uq[:, :, 0:128].reshape(384, 768)
    ropec = uq[:, :, 128:192]
    swp = np.concatenate([ropec[:, :, 32:64], ropec[:, :, 0:32]], axis=2)
    sh["wq"] = _chunked(np.concatenate([nope, ropec.reshape(384, 384), swp.reshape(384, 384)], axis=1))
    ukv = inp["mla_w_ukv"][0].reshape(256, 6, 256)
    sh["wkv"] = _chunked(np.concatenate([ukv[:, :, 0:128].reshape(256, 768), ukv[:, :, 128:256].reshape(256, 768)], axis=1))
    gains = np.zeros((128, NGAIN), f)
    gains[:, 0:8] = _cols(inp["norm_g"][0])
    gains[:, 8:16] = _cols(inp["norm_g"][1])
    gains[:, 16:24] = _cols(inp["final_norm_g"])
    gains[:, 24:32] = _cols(inp["mem_norm_g"])
    gains[:, 32:38] = _cols(inp["pool_scale"][0])
    gains[:, 38:41] = _cols(inp["mla_q_norm_g"][0])
    gains[:, 41:43] = _cols(inp["mla_kv_norm_g"][0])
    sh["gains"] = gains
    cst = np.zeros((128, NCST), f)
    invf = (f(10000.0) ** (-(np.arange(0, 64, 2, dtype=f) / f(64)))).astype(f)
    p = np.arange(128)
    cst[:, 0] = invf[p % 32]
    cst[:, 1] = np.pi / 2
    cst[:, 2] = np.where((p % 64) < 32, np.pi, 0.0)
    for li, w_ in enumerate((2, 4, 8, 16)):
        cst[:, 8 + 16 * li:8 + 16 * li + 16] = 1.0 / np.minimum(np.arange(16) + 1, w_)
    sh["cst"] = cst
    sh["ident"] = np.eye(128, dtype=f)
    NEG = -30000.0
    mneg = np.zeros((128, 2, 256), f)
    mneg[64:128, 0, 0:64] = NEG
    mneg[:, 1, 0:128] = NEG
    mneg[64:128, 1, 128:192] = NEG
    sh["mneg"] = mneg
    return {k: np.ascontiguousarray(v, dtype=f) for k, v in sh.items()}


_NC_CACHE = {}


def kernel(**inp):
    inp = {k: np.asarray(v) for k, v in inp.items()}
    x, mem, positions = inp["x"], inp["mem"], inp["positions"]
    Bt, S, _ = x.shape
    ncores = 8
    NB = Bt // ncores
    sh = prep_shared(inp)
    key = (S, NB)
    if key not in _NC_CACHE:
        _NC_CACHE[key] = build_nc(S=S, NB=NB)
    nc = _NC_CACHE[key]
    in_maps = []
    for c in range(ncores):
        sl = slice(c * NB, (c + 1) * NB)
        m = dict(sh)
        m["xT"] = np.ascontiguousarray(x[sl].transpose(0, 2, 1))
        m["memT"] = np.ascontiguousarray(mem[sl].transpose(0, 2, 1))
        m["pos"] = np.ascontiguousarray(positions[sl].astype(np.int32))
        in_maps.append(m)
    res = run_bass_kernel_spmd(nc, in_maps, core_ids=list(range(ncores)))
    out = np.concatenate([np.asarray(r["outT"]).transpose(0, 2, 1) for r in res.results], axis=0)
    return np.ascontiguousarray(out, dtype=np.float32)
```
